# Optimizing a Trainium2 kernel written in Bass

```python
import jax, jax.numpy as jnp
from jax import lax
import numpy as np

D_MODEL = 1024
BATCH = 8
SEQ = 2048
DEPTH = 1
DEC_BATCH = 128
DEC_SEQ = 8
PAST_LEN = 16384
PAGE_SIZE = 128

N_MEM = 256
GLA_HEADS = 4
GLA_DK = D_MODEL // 2 // GLA_HEADS
GLA_DV = D_MODEL // GLA_HEADS
GLA_QK = GLA_HEADS * GLA_DK
GLA_V = GLA_HEADS * GLA_DV
GLA_GATE_RANK = 16
GLA_TAU = 16.0
RET_HEADS = 4
RET_DK = D_MODEL // 2 // RET_HEADS
RET_DV = D_MODEL // RET_HEADS
RET_QK = RET_HEADS * RET_DK
RET_V = RET_HEADS * RET_DV
XA_HEADS = 4
XA_DH = D_MODEL // XA_HEADS
D_FF = 4 * D_MODEL
CHUNK = 64
ROPE_BASE = 10000.0
EPS = 1e-6
IN_WIDTHS = (GLA_QK, GLA_QK, GLA_V, GLA_V, GLA_GATE_RANK, RET_QK, RET_QK, RET_V, RET_V, D_MODEL, D_MODEL)
N_IN = 2 * GLA_QK + 2 * GLA_V + GLA_GATE_RANK + 2 * RET_QK + 2 * RET_V + 2 * D_MODEL

kernel_name = "gla_retnet_parallel_memxattn_decoder_step"


def rmsnorm(x, g):
    xf = x.astype(jnp.float32)
    y = xf * lax.rsqrt(jnp.mean(xf * xf, axis=-1, keepdims=True) + EPS)
    return (y * g.astype(jnp.float32)).astype(x.dtype)


def groupnorm(x, g):
    xf = x.astype(jnp.float32)
    mu = jnp.mean(xf, axis=-1, keepdims=True)
    xc = xf - mu
    y = xc * lax.rsqrt(jnp.mean(xc * xc, axis=-1, keepdims=True) + EPS)
    return (y * g.astype(jnp.float32)).astype(x.dtype)


def rotary(x, pos):
    half = x.shape[-1] // 2
    inv = ROPE_BASE ** (-jnp.arange(half, dtype=jnp.float32) / half)
    ang = pos.astype(jnp.float32)[:, None] * inv[None, :]
    cos = jnp.cos(ang)[None, :, None, :]
    sin = jnp.sin(ang)[None, :, None, :]
    xf = x.astype(jnp.float32)
    x1, x2 = xf[..., :half], xf[..., half:]
    return jnp.concatenate([x1 * cos - x2 * sin, x1 * sin + x2 * cos], axis=-1).astype(x.dtype)


def chunked_decay_linear_attn(q, k, v, log_a, s0):
    B, H, L, dk = q.shape
    c = min(CHUNK, L)
    n = -(-L // c)
    pad = n * c - L

    def prep(t):
        t = t.astype(jnp.float32)
        t = jnp.pad(t, ((0, 0), (0, 0), (0, pad), (0, 0)))
        t = t.reshape(B, H, n, c, t.shape[-1])
        return jnp.moveaxis(t, 2, 0)

    qs, ks, vs, as_ = prep(q), prep(k), prep(v), prep(log_a)
    mask = jnp.tril(jnp.ones((c, c), dtype=bool))

    def step(S, inp):
        qc, kc, vc, ac = inp
        b = jnp.cumsum(ac, axis=2)
        qe = qc * jnp.exp(b)
        ke = kc * jnp.exp(-b)
        scores = jnp.where(mask, jnp.einsum('bhtd,bhsd->bhts', qe, ke), 0.0)
        o = jnp.einsum('bhts,bhsv->bhtv', scores, vc) + jnp.einsum('bhtd,bhdv->bhtv', qe, S)
        b_last = b[:, :, -1:, :]
        kd = kc * jnp.exp(b_last - b)
        S_new = jnp.exp(b_last)[:, :, 0, :, None] * S + jnp.einsum('bhsd,bhsv->bhdv', kd, vc)
        return S_new, o

    S_fin, os_ = lax.scan(step, s0.astype(jnp.float32), (qs, ks, vs, as_))
    o = jnp.moveaxis(os_, 0, 2).reshape(B, H, n * c, -1)[:, :, :L]
    return o, S_fin.astype(s0.dtype)


def memory_kv(mem, g_mem, w_xk, w_xv):
    B = mem.shape[0]
    m = rmsnorm(mem, g_mem)
    k = (m @ w_xk).reshape(B, N_MEM, XA_HEADS, XA_DH)
    v = (m @ w_xv).reshape(B, N_MEM, XA_HEADS, XA_DH)
    return k, v


def mixer_block(h, pos, s_gla, s_ret, w_in, w_gla_a2, b_gla_a, g_gla_head, g_ret_head, w_mix_out):
    B, L, _ = h.shape
    z = h @ w_in
    (gq, gk, gv, gr, ga_low, rq, rk, rv, rg, gate_a, gate_b) = jnp.split(
        z, [int(s) for s in np.cumsum(IN_WIDTHS)[:-1]], axis=-1)

    def to_heads(t, H):
        return t.reshape(B, L, H, -1).transpose(0, 2, 1, 3)

    log_alpha = jax.nn.log_sigmoid((ga_low @ w_gla_a2 + b_gla_a).astype(jnp.float32)) / GLA_TAU
    o_g, s_gla_new = chunked_decay_linear_attn(
        to_heads(gq, GLA_HEADS) * (GLA_DK ** -0.5), to_heads(gk, GLA_HEADS), to_heads(gv, GLA_HEADS),
        to_heads(log_alpha, GLA_HEADS), s_gla)
    o_g = rmsnorm(o_g.transpose(0, 2, 1, 3).astype(h.dtype), g_gla_head).reshape(B, L, GLA_V)
    o_g = o_g * jax.nn.silu(gr)

    rq_h = rotary(rq.reshape(B, L, RET_HEADS, RET_DK), pos).transpose(0, 2, 1, 3)
    rk_h = rotary(rk.reshape(B, L, RET_HEADS, RET_DK), pos).transpose(0, 2, 1, 3) * (RET_DK ** -0.5)
    log_gamma = jnp.log1p(-jnp.exp2(-5.0 - jnp.arange(RET_HEADS, dtype=jnp.float32)))
    log_dec = jnp.broadcast_to(log_gamma[None, :, None, None], (B, RET_HEADS, L, 1))
    o_r, s_ret_new = chunked_decay_linear_attn(rq_h, rk_h, to_heads(rv, RET_HEADS), log_dec, s_ret)
    o_r = groupnorm(o_r.transpose(0, 2, 1, 3).astype(h.dtype), g_ret_head).reshape(B, L, RET_V)
    o_r = o_r * jax.nn.silu(rg)

    merged = jax.nn.sigmoid(gate_a) * o_g + jax.nn.sigmoid(gate_b) * o_r
    return merged @ w_mix_out, s_gla_new, s_ret_new


def cross_attn(h, mem_k, mem_v, w_xq, w_xo):
    B, L, _ = h.shape
    q = (h @ w_xq).reshape(B, L, XA_HEADS, XA_DH)
    s = jnp.einsum('blhd,bmhd->bhlm', q, mem_k).astype(jnp.float32) * (XA_DH ** -0.5)
    p = jax.nn.softmax(s, axis=-1).astype(h.dtype)
    o = jnp.einsum('bhlm,bmhd->blhd', p, mem_v).reshape(B, L, D_MODEL)
    return o @ w_xo


def layer(x, pos, s_gla, s_ret, mem_k, mem_v, w_in, w_gla_a2, b_gla_a, g_gla_head, g_ret_head, w_mix_out,
          w_xq, w_xo, w_up, w_down, g_pre_mix, g_post_mix, g_pre_xa, g_post_xa, g_pre_ffn, g_post_ffn):
    m, s_gla_new, s_ret_new = mixer_block(rmsnorm(x, g_pre_mix), pos, s_gla, s_ret, w_in, w_gla_a2, b_gla_a,
                                          g_gla_head, g_ret_head, w_mix_out)
    x = x + rmsnorm(m, g_post_mix)
    x = x + rmsnorm(cross_attn(rmsnorm(x, g_pre_xa), mem_k, mem_v, w_xq, w_xo), g_post_xa)
    hf = rmsnorm(x, g_pre_ffn)
    x = x + rmsnorm(jnp.square(jax.nn.relu(hf @ w_up)) @ w_down, g_post_ffn)
    return x, s_gla_new, s_ret_new


def setup_inputs(seed: int = 0) -> dict:
    key = jax.random.key(seed)
    ks = jax.random.split(key, 32)
    f32 = jnp.float32

    def w(k, shape, fan_in):
        return jax.random.normal(k, shape, f32) * (fan_in ** -0.5)

    def gain(k, shape):
        return 1.0 + 0.01 * jax.random.normal(k, shape, f32)

    return {
        "x_prompt": jax.random.normal(ks[0], (BATCH, SEQ, D_MODEL), f32),
        "x_sample": jax.random.normal(ks[1], (DEC_BATCH, DEC_SEQ, D_MODEL), f32),
        "state_gla": 0.1 * jax.random.normal(ks[2], (DEPTH, DEC_BATCH, GLA_HEADS, GLA_DK, GLA_DV), f32),
        "state_ret": 0.1 * jax.random.normal(ks[3], (DEPTH, DEC_BATCH, RET_HEADS, RET_DK, RET_DV), f32),
        "cache_mem_k": jax.random.normal(ks[4], (DEPTH, DEC_BATCH, N_MEM, XA_HEADS, XA_DH), f32),
        "cache_mem_v": jax.random.normal(ks[5], (DEPTH, DEC_BATCH, N_MEM, XA_HEADS, XA_DH), f32),
        "mem_prompt": jax.random.normal(ks[6], (BATCH, N_MEM, D_MODEL), f32),
        "w_in": w(ks[7], (DEPTH, D_MODEL, N_IN), D_MODEL),
        "w_gla_a2": w(ks[8], (DEPTH, GLA_GATE_RANK, GLA_QK), GLA_GATE_RANK),
        "b_gla_a": 0.01 * jax.random.normal(ks[9], (DEPTH, GLA_QK), f32),
        "g_gla_head": gain(ks[10], (DEPTH, GLA_HEADS, GLA_DV)),
        "g_ret_head": gain(ks[11], (DEPTH, RET_HEADS, RET_DV)),
        "w_mix_out": w(ks[12], (DEPTH, D_MODEL, D_MODEL), D_MODEL),
        "w_xq": w(ks[13], (DEPTH, D_MODEL, D_MODEL), D_MODEL),
        "w_xk": w(ks[14], (DEPTH, D_MODEL, D_MODEL), D_MODEL),
        "w_xv": w(ks[15], (DEPTH, D_MODEL, D_MODEL), D_MODEL),
        "w_xo": w(ks[16], (DEPTH, D_MODEL, D_MODEL), D_MODEL),
        "g_mem": gain(ks[17], (DEPTH, D_MODEL)),
        "w_up": w(ks[18], (DEPTH, D_MODEL, D_FF), D_MODEL),
        "w_down": w(ks[19], (DEPTH, D_FF, D_MODEL), D_FF),
        "g_pre_mix": gain(ks[20], (DEPTH, D_MODEL)),
        "g_post_mix": gain(ks[21], (DEPTH, D_MODEL)),
        "g_pre_xa": gain(ks[22], (DEPTH, D_MODEL)),
        "g_post_xa": gain(ks[23], (DEPTH, D_MODEL)),
        "g_pre_ffn": gain(ks[24], (DEPTH, D_MODEL)),
        "g_post_ffn": gain(ks[25], (DEPTH, D_MODEL)),
    }


def reference(x_prompt, x_sample, state_gla, state_ret, cache_mem_k, cache_mem_v, mem_prompt,
              w_in, w_gla_a2, b_gla_a, g_gla_head, g_ret_head, w_mix_out,
              w_xq, w_xk, w_xv, w_xo, g_mem, w_up, w_down,
              g_pre_mix, g_post_mix, g_pre_xa, g_post_xa, g_pre_ffn, g_post_ffn):
    Bp, Lp, _ = x_prompt.shape
    pos_p = jnp.arange(Lp, dtype=jnp.int32)
    pos_s = PAST_LEN + jnp.arange(x_sample.shape[1], dtype=jnp.int32)
    zero_gla = jnp.zeros((Bp, GLA_HEADS, GLA_DK, GLA_DV), jnp.float32)
    zero_ret = jnp.zeros((Bp, RET_HEADS, RET_DK, RET_DV), jnp.float32)

    yp, ys = x_prompt, x_sample
    gla_p, ret_p, mk_p, mv_p, gla_s, ret_s = [], [], [], [], [], []
    for l in range(DEPTH):
        shared = (w_in[l], w_gla_a2[l], b_gla_a[l], g_gla_head[l], g_ret_head[l], w_mix_out[l],
                  w_xq[l], w_xo[l], w_up[l], w_down[l],
                  g_pre_mix[l], g_post_mix[l], g_pre_xa[l], g_post_xa[l], g_pre_ffn[l], g_post_ffn[l])
        mk, mv = memory_kv(mem_prompt, g_mem[l], w_xk[l], w_xv[l])
        yp, sg, sr = layer(yp, pos_p, zero_gla, zero_ret, mk, mv, *shared)
        gla_p.append(sg); ret_p.append(sr); mk_p.append(mk); mv_p.append(mv)
        ys, sg2, sr2 = layer(ys, pos_s, state_gla[l], state_ret[l], cache_mem_k[l], cache_mem_v[l], *shared)
        gla_s.append(sg2); ret_s.append(sr2)

    new_state_gla_prompt = jnp.stack(gla_p)
    new_state_ret_prompt = jnp.stack(ret_p)
    new_cache_mem_k_prompt = jnp.stack(mk_p)
    new_cache_mem_v_prompt = jnp.stack(mv_p)
    new_state_gla_sample = jnp.stack(gla_s)
    new_state_ret_sample = jnp.stack(ret_s)
    return (yp, ys, new_state_gla_prompt, new_state_ret_prompt, new_cache_mem_k_prompt, new_cache_mem_v_prompt,
            new_state_gla_sample, new_state_ret_sample)
```

```python
import numpy as np
import os as _os
from contextlib import ExitStack
import concourse.bass as bass
import concourse.mybir as mybir
from concourse.bass_utils import run_bass_kernel_spmd

F32 = mybir.dt.float32
BF16 = mybir.dt.bfloat16
AF = mybir.ActivationFunctionType
ALU = mybir.AluOpType

D = 1024
NIN = 8208
EPS = 1e-6
NCORE = 8
LP = 2048
NSB = 16
PAST = 16384
O_GQ, O_GK, O_GV, O_GR, O_GA, O_RQ, O_RK, O_RV, O_RG, O_A, O_B = (
    0, 512, 1024, 2048, 3072, 3088, 3600, 4112, 5136, 6160, 7184)
SLOTW = 1280


class Buf:
    __slots__ = ("name", "w", "rs", "excl")

    def __init__(self, name):
        self.name = name
        self.w = None
        self.rs = []
        self.excl = False


class _Rec:
    def __getattr__(self, name):
        def f(*a, **kw):
            return (name, a, kw)
        return f


_REC = _Rec()


def _replay(engine, rec):
    return getattr(engine, rec[0])(*rec[1], **rec[2])


class Sched:
    ENG = ("pe", "act", "dve", "pool", "sp")

    def __init__(self, nc, n_dma_sems=12):
        self.nc = nc
        self.q = {e: [] for e in self.ENG}
        self.sem = {}
        self.semname = {}
        self.cnt = {e: 0 for e in self.ENG}
        for e in self.ENG:
            self.sem[e] = nc.alloc_semaphore(name="s_" + e)
            self.semname[id(self.sem[e])] = e
        self.dsem = {}
        for e in ("sp", "pool"):
            self.dsem[e] = [[nc.alloc_semaphore(name="d_%s%d" % (e, i)), 0]
                            for i in range(n_dma_sems if e == "sp" else 4)]
        self.dnext = {e: 0 for e in ("sp", "pool")}
        self.waited = {e: {} for e in self.ENG}
        self.out_tokens = []
        self.tag = ""
        self.pelog = []
        self._pend_waits = []

    def _need(self, eng, tok):
        if tok is None:
            return
        sem, val = tok
        key = id(sem)
        if sem is self.sem[eng] and eng == "pe":
            return
        if self.waited[eng].get(key, 0) >= val:
            return
        self.waited[eng][key] = val
        self.q[eng].append(("wait", sem, val))
        if eng == "pe":
            self._pend_waits.append((self.semname.get(key, "dma"), val))

    @staticmethod
    def _flat(bufs):
        out = []
        for b in bufs:
            if isinstance(b, (list, tuple)):
                out.extend(b)
            else:
                out.append(b)
        return out

    def _deps(self, eng, reads, writes):
        for b in reads:
            self._need(eng, b.w)
        for b in writes:
            self._need(eng, b.w)
            for t in b.rs:
                self._need(eng, t)

    def _commit(self, tok, reads, writes):
        for b in writes:
            b.w = tok
            b.rs = []
        for b in reads:
            b.rs.append(tok)
            if len(b.rs) > 48:
                b.rs = b.rs[-48:]

    def op(self, eng, fn, reads=(), writes=()):
        reads = self._flat(reads); writes = self._flat(writes)
        writes = list(writes) + [b for b in reads if b.excl]
        self._deps(eng, reads, writes)
        self.cnt[eng] += 1
        tok = (self.sem[eng], self.cnt[eng])
        self.q[eng].append(("op", fn(_REC), self.sem[eng]))
        self._commit(tok, reads, writes)
        return tok

    def pe(self, fns, reads=(), writes=()):
        eng = "pe"
        reads = self._flat(reads); writes = self._flat(writes)
        self._deps(eng, reads, writes)
        for k_, fn in enumerate(fns):
            self.pelog.append((self.tag, self._pend_waits if k_ == 0 else []))
        self._pend_waits = []
        for fn in fns[:-1]:
            self.q[eng].append(("op", fn(_REC), None))
        self.cnt[eng] += 1
        tok = (self.sem[eng], self.cnt[eng])
        self.q[eng].append(("op", fns[-1](_REC), self.sem[eng]))
        self._commit(tok, reads, writes)
        return tok

    def dma(self, eng, out, in_, reads=(), writes=(), is_output=False, **kw):
        reads = self._flat(reads); writes = self._flat(writes)
        self._deps(eng, reads, writes)
        lst = self.dsem[eng]
        i = self.dnext[eng]
        self.dnext[eng] = (i + 1) % len(lst)
        sem, cur = lst[i]
        if cur > 0:
            self._need(eng, (sem, cur))
        lst[i][1] = cur + 16
        tok = (sem, cur + 16)

        def fn(e, out=out, in_=in_, kw=kw):
            return e.dma_start(out=out, in_=in_, **kw)
        self.q[eng].append(("dma", fn(_REC), sem))
        self._commit(tok, reads, writes)
        if is_output:
            self.out_tokens.append(tok)
        return tok

    def barrier(self):
        toks = []
        for e in self.ENG:
            if self.cnt[e] > 0:
                toks.append((self.sem[e], self.cnt[e]))
        for e in ("sp",):
            for sem, cur in self.dsem[e]:
                if cur > 0:
                    toks.append((sem, cur))
        for e in self.ENG:
            for t in toks:
                self._need(e, t)

    def finish(self):
        for t in self.out_tokens:
            self._need("sp", t)
        for e in self.ENG:
            if e != "sp" and self.cnt[e] > 0:
                self._need("sp", (self.sem[e], self.cnt[e]))

    def emit(self):
        nc = self.nc
        engmap = {"pe": "tensor", "act": "scalar", "dve": "vector",
                  "pool": "gpsimd", "sp": "sync"}
        with nc.Block() as block:
            for e in self.ENG:
                items = self.q[e]

                def body(engine, items=items):
                    for it in items:
                        if it[0] == "wait":
                            engine.wait_ge(it[1], it[2])
                        elif it[0] == "op":
                            ins = _replay(engine, it[1])
                            if it[2] is not None:
                                ins.then_inc(it[2], 1)
                        else:
                            ins = _replay(engine, it[1])
                            ins.then_inc(it[2], 16)
                getattr(block, engmap[e])(body)


class _Stop(Exception):
    pass


def _stop_if(n):
    v = _os.environ.get("KDBG_STOP")
    if v is not None and int(v) == n:
        raise _Stop()


class TB:
    def __init__(self, t, name, nb=1):
        self.t = t
        self.b = Buf(name)
        self.bl = [Buf(name + str(i)) for i in range(nb)]


def _consts():
    c = {}
    c["ident"] = np.eye(128, dtype=np.float32)
    c["ones"] = np.ones((128, 128), dtype=np.float32)
    lg = np.log1p(-np.exp2(-5.0 - np.arange(4, dtype=np.float64)))
    for kind, cs in (("p", 128), ("s", 8)):
        nch = 128 // cs
        s = np.arange(128)
        cid = s // cs
        tin = s % cs
        same = cid[:, None] == cid[None, :]
        mcum = (-1.0 / 16) * (same & (s[:, None] <= s[None, :]))
        ilast = (-1.0 / 16) * (cid[:, None] == np.arange(nch)[None, :])
        c["mcum_" + kind] = np.concatenate([mcum, ilast], 1).astype(np.float32)
        c["mrem_" + kind] = ((-1.0 / 16) * (same & (s[:, None] > s[None, :]))).astype(np.float32)
        c["caus_" + kind] = (1.0 * (same & (s[:, None] <= s[None, :]))).astype(np.float32)
        ind = 1.0 * (cid[:, None] == np.arange(nch)[None, :])
        c["ind_" + kind] = ind.astype(np.float32)
        gq = np.zeros((4, 128, 128)); gk = np.zeros((4, 128, 128))
        wk = np.zeros((4, 128, nch)); ed = np.zeros((128, 4))
        for h in range(4):
            gq[h] = np.exp(lg[h] * (tin + 1))[None, :]
            gk[h] = (128.0 ** -0.5) * np.exp(-lg[h] * (tin + 1))[None, :]
            wk[h] = np.exp(lg[h] * cs) * ind
            ed[:, h] = np.exp(lg[h] * cs)
        c["gq_" + kind] = np.ascontiguousarray(gq.transpose(1, 0, 2)).astype(np.float32)
        c["gk_" + kind] = np.ascontiguousarray(gk.transpose(1, 0, 2)).astype(np.float32)
        c["wk_" + kind] = np.ascontiguousarray(wk.transpose(1, 0, 2)).astype(np.float32)
        c["ed_" + kind] = ed.astype(np.float32)
    half = 64
    inv = (np.float32(10000.0) ** (-np.arange(half, dtype=np.float32) / np.float32(half))).astype(np.float32)
    pos = np.concatenate([np.arange(LP), PAST + (np.arange(128) % 8)]).astype(np.float32)
    ang = (pos[:, None] * inv[None, :]).astype(np.float32).astype(np.float64)
    cs_ = np.cos(ang).T; sn_ = np.sin(ang).T
    c["cosT"] = np.concatenate([cs_, cs_], 0).astype(np.float32)
    c["sinT"] = np.concatenate([-sn_, sn_], 0).astype(np.float32)
    return c


CONST_SHAPES = None


def build_nc():
    consts = _consts()
    nc = bass.Bass("TRN2", target_bir_lowering=False)

    def din(name, shape):
        return nc.dram_tensor(name, list(shape), F32, kind="ExternalInput")

    def dout(name, shape):
        return nc.dram_tensor(name, list(shape), F32, kind="ExternalOutput")

    xp = din("xp", [LP, D]); xsm = din("xsm", [128, D])
    sgla = din("sgla", [NSB, 4, 128, 256]); sret = din("sret", [NSB, 4, 128, 256])
    cmk = din("cmk", [NSB, 256, D]); cmv = din("cmv", [NSB, 256, D])
    memp = din("memp", [256, D])
    w_in = din("w_inh", [8, D, SLOTW]); w_a2 = din("w_a2", [16, 512]); b_a = din("b_a", [1, 512])
    g_gh = din("g_gh", [1, D]); g_rh = din("g_rh", [1, D])
    w_mo = din("w_mo", [D, D]); w_xq = din("w_xq", [D, D]); w_xk = din("w_xk", [D, D])
    w_xv = din("w_xv", [D, D]); w_xo = din("w_xo", [D, D])
    w_up = din("w_up", [D, 4 * D]); w_dn = din("w_dn", [4 * D, D])
    gvec = {n: din(n, [1, D]) for n in ("g_mem", "g_pre_mix", "g_post_mix", "g_pre_xa",
                                        "g_post_xa", "g_pre_ffn", "g_post_ffn")}
    cd = {k: din("c_" + k, v.shape) for k, v in consts.items()}

    wsc = nc.dram_tensor("wsc", [19, 128, 8, SLOTW], BF16, kind="Internal")
    yp = dout("yp", [LP, D]); ysm = dout("ysm", [128, D])
    oglap = dout("oglap", [4, 128, 256]); oretp = dout("oretp", [4, 128, 256])
    omk = dout("omk", [256, D]); omv = dout("omv", [256, D])
    oglas = dout("oglas", [NSB, 4, 128, 256]); orets = dout("orets", [NSB, 4, 128, 256])

    S = Sched(nc)
    st = ExitStack()
    with st:
      try:
        _build_body(nc, S, st, consts, locals())
      except _Stop:
        pass
      S.finish()
      S.emit()
    return nc, consts


def _build_body(nc, S, st, consts, L):
    (xp, xsm, sgla, sret, cmk, cmv, memp, w_in, w_a2, b_a, g_gh, g_rh, w_mo, w_xq, w_xk, w_xv, w_xo, w_up, w_dn,
     gvec, cd, yp, ysm, oglap, oretp, omk, omv, oglas, orets, wsc) = [L[k] for k in (
        "xp", "xsm", "sgla", "sret", "cmk", "cmv", "memp", "w_in", "w_a2", "b_a", "g_gh", "g_rh", "w_mo", "w_xq", "w_xk",
        "w_xv", "w_xo", "w_up", "w_dn", "gvec", "cd", "yp", "ysm", "oglap", "oretp", "omk", "omv", "oglas", "orets", "wsc")]
    if True:
        cnt = [0]

        def sb(shape, dt, nb=1, stack=st):
            cnt[0] += 1
            nm = "t%d" % cnt[0]
            return TB(stack.enter_context(nc.sbuf_tensor(nm, list(shape), dt)), nm, nb)

        PSD = [st.enter_context(nc.psum_tensor("ps%d" % i, [128, 1024], F32)) for i in range(4)]
        PSB = [Buf("psb%d" % i) for i in range(8)]
        for b_ in PSB:
            b_.excl = True
        psi = [0]

        class PV:
            def __init__(self, t, b):
                self.t = t
                self.b = b

        def ps():
            i = psi[0] % 8
            psi[0] += 1
            return PV(PSD[i // 2][:, (i % 2) * 512:(i % 2) * 512 + 512], PSB[i])

        def ps2():
            if psi[0] % 2 == 1:
                psi[0] += 1
            i = psi[0] % 8
            psi[0] += 2
            return PV(PSD[i // 2], [PSB[i], PSB[i + 1]])

        slots = [sb([128, 8, SLOTW], BF16) for _ in range(3)]
        for sl_ in slots:
            S.op("dve", lambda e, sl_=sl_: e.memset(sl_.t[:], 0.0), writes=[sl_.b])
            w0_ = sl_.b.w
            sl_.parts = [Buf(sl_.b.name + "_p%d" % j_) for j_ in range(12)]
            for p_ in sl_.parts:
                p_.w = w0_
            sl_.b = sl_.parts
            sl_.pi = 0
        sli = [0]

        def slot():
            s_ = slots[sli[0] % 4]
            sli[0] += 1
            return s_

        wplan = []
        wstate = {"next": 0, "issued": 0, "released": 0}

        wscb = [Buf("wsc%d" % i_) for i_ in range(19)]

        def w_issue(idx):
            ent = wplan[idx]
            sl = slots[idx % 3]
            sl.pi = 0
            kind = ent[0]
            if idx >= 2 and (idx - 2) // 19 >= 1:
                e_ = (idx - 2) % 19
                S.dma("pool", sl.t[:], wsc.ap()[e_], reads=[wscb[e_]], writes=[sl.parts[0]])
                sl.pi = 1
                return
            if kind == "full":
                wload(sl, 0, ent[1], 0, 0, D)
            elif kind == "ffn":
                jb = ent[1]
                wload(sl, 0, w_up, 0, jb * 512, 512)
                for hf in range(2):
                    vdn = w_dn.ap()[jb * 512:(jb + 1) * 512, hf * 512:(hf + 1) * 512].rearrange("(fc p) n -> p fc n", p=128)
                    S.dma("pool", sl.t[:, :, 512:1024].rearrange("p (fc hf) n -> p fc hf n", hf=2)[:, :, hf, :], vdn, writes=[sl.parts[sl.pi]])
                    sl.pi += 1
            else:
                hh = ent[1]
                ncol = SLOTW if hh % 2 == 1 else (1040 if hh == 0 else 1024)
                v = w_in.ap()[hh, :, 0:ncol].rearrange("(k p) n -> p k n", p=128)
                S.dma("pool", sl.t[:, :, 0:ncol], v, writes=[sl.parts[sl.pi]])
                sl.pi += 1

        def w_pump():
            while wstate["issued"] < len(wplan) and wstate["issued"] - wstate["released"] < 3:
                w_issue(wstate["issued"])
                wstate["issued"] += 1

        def wnext(expect):
            idx = wstate["next"]
            assert wplan[idx][0] == expect, (wplan[idx], expect)
            wstate["next"] += 1
            w_pump()
            assert wstate["issued"] > idx, "weight slot ring exhausted"
            if idx >= 2 and (idx - 2) // 19 == 0:
                e_ = (idx - 2) % 19
                S.dma("sp", wsc.ap()[e_], slots[idx % 3].t[:], reads=[slots[idx % 3].b], writes=[wscb[e_]])
            return slots[idx % 3]

        def wdone():
            wstate["released"] += 1
            w_pump()

        def wload(sl, col0, src, r0, c0, ncol, nk=8):
            v = src.ap()[r0:r0 + nk * 128, c0:c0 + ncol].rearrange("(k p) n -> p k n", p=128)
            S.dma("pool", sl.t[:, 0:nk, col0:col0 + ncol], v, writes=[sl.parts[sl.pi]])
            sl.pi += 1

        wplan.append(("full", w_xk)); wplan.append(("full", w_xv))
        for _g in range(5):
            for _hh in range(8):
                wplan.append(("head", _hh))
            wplan.append(("full", w_mo)); wplan.append(("full", w_xq)); wplan.append(("full", w_xo))
            for _jb in range(8):
                wplan.append(("ffn", _jb))

        cs = {}
        for k, v in consts.items():
            if k in ("cosT", "sinT"):
                continue
            cs[k] = sb(list(v.shape), F32)
            S.dma("sp", cs[k].t[:], cd[k].ap(), writes=[cs[k].b])
        identb = sb([128, 128], BF16)
        S.op("dve", lambda e: e.tensor_copy(out=identb.t[:], in_=cs["ident"].t[:]), reads=[cs["ident"].b], writes=[identb.b])
        onesb = sb([128, 128], BF16)
        S.op("dve", lambda e: e.tensor_copy(out=onesb.t[:], in_=cs["ones"].t[:]), reads=[cs["ones"].b], writes=[onesb.b])
        causb = {}
        for kind in ("p", "s"):
            causb[kind] = cs["caus_" + kind]
        gcol = {}
        g8 = sb([8, 4, 128], F32)
        for j_, n in enumerate(("g_mem", "g_pre_mix", "g_pre_xa", "g_pre_ffn")):
            S.dma("sp", g8.t[:, j_, :], gvec[n].ap().rearrange("o (k p) -> (o k) p", p=128), writes=[g8.b])
        for j_, n in enumerate(("g_mem", "g_pre_mix", "g_pre_xa", "g_pre_ffn")):
            gcol[n] = sb([128, 8], F32)
            pg = ps()
            S.pe([lambda e, j_=j_, pg=pg: e.matmul(out=pg.t[:, 0:8], lhsT=g8.t[:, j_, :], rhs=cs["ident"].t[0:8, 0:8], start=True, stop=True)],
                 reads=[g8.b, cs["ident"].b], writes=[pg.b])
            S.op("dve", lambda e, n=n, pg=pg: e.tensor_copy(out=gcol[n].t[:], in_=pg.t[:, 0:8]), reads=[pg.b], writes=[gcol[n].b])
        gB = {}
        gsrc = {"g_post_mix": gvec["g_post_mix"], "g_post_xa": gvec["g_post_xa"],
                "g_post_ffn": gvec["g_post_ffn"], "g_gh": g_gh, "g_rh": g_rh}

        def load_gB(n, stack):
            gB[n] = sb([128, D], F32, stack=stack)
            S.dma("sp", gB[n].t[:], bass.AP(gsrc[n], 0, [[0, 128], [1, D]]), writes=[gB[n].b])
        wa2 = sb([16, 512], F32)
        S.dma("sp", wa2.t[:], w_a2.ap(), writes=[wa2.b])
        bga = sb([1, 512], F32)
        S.dma("sp", bga.t[:], b_a.ap(), writes=[bga.b])
        Sst = sb([128, 8, 256], F32, nb=8)
        Sbf = [sb([128, 8, 256], BF16, nb=8) for _ in range(2)]
        S.op("dve", lambda e: e.memset(Sst.t[:], 0.0), writes=[Sst.b] + Sst.bl)
        S.op("dve", lambda e: e.memset(Sbf[0].t[:], 0.0), writes=[Sbf[0].b] + Sbf[0].bl)
        sbf_cur = [0] * 8
        KTp = sb([128, 8, 256], BF16)
        Vp = sb([128, 2, D], BF16)
        xs = sb([128, 4, D], F32, nb=4)
        junk = sb([128, D], BF16)
        ssv = sb([128, 64], F32)
        ssb = [Buf("ss%d" % i) for i in range(64)]
        ssi = [0]

        def sscol():
            i = ssi[0] % 64
            ssi[0] += 1
            return PV(ssv.t[:, i:i + 1], ssb[i])

        def rstd_from(ss, n):
            t2 = sscol()
            S.op("act", lambda e: e.activation(out=t2.t, in_=ss.t, func=AF.Ln, scale=1.0 / n, bias=EPS), reads=[ss.b], writes=[t2.b])
            S.op("act", lambda e: e.activation(out=t2.t, in_=t2.t, func=AF.Exp, scale=-0.5), reads=[t2.b], writes=[t2.b])
            return t2

        def norm_T(src_ap, src_bufs, g_c, dstT, col0, xnb):
            ss = sscol()
            S.op("act", lambda e: e.activation(out=junk.t[:], in_=src_ap, func=AF.Square, accum_out=ss.t),
                 reads=src_bufs, writes=[ss.b])
            r = rstd_from(ss, D)
            S.op("dve", lambda e: e.tensor_scalar(out=xnb.t[:], in0=src_ap, scalar1=r.t, scalar2=None, op0=ALU.mult),
                 reads=src_bufs + [r.b], writes=[xnb.b])
            p = ps()
            pv = p.t[:].bitcast(BF16)
            S.pe([(lambda e, k=k: e.transpose(out=pv[:, k * 128:(k + 1) * 128], in_=xnb.t[:, k * 128:(k + 1) * 128], identity=identb.t[:]))
                  for k in range(8)], reads=[xnb.b, identb.b], writes=[p.b])
            S.op("dve", lambda e: e.tensor_tensor(out=dstT.t[:, :, col0:col0 + 128],
                                                  in0=pv[:, 0:1024].rearrange("p (k t) -> p k t", k=8),
                                                  in1=g_c.t[:].unsqueeze(2).to_broadcast([128, 8, 128]), op=ALU.mult),
                 reads=[p.b, g_c.b], writes=[dstT.b])

        def post_norm_add(p, gname, xi, xbuf, tmp):
            ss = sscol()
            S.op("act", lambda e: e.activation(out=junk.t[:], in_=p.t[:], func=AF.Square, accum_out=ss.t),
                 reads=[p.b], writes=[ss.b])
            r = rstd_from(ss, D)
            S.op("dve", lambda e: e.scalar_tensor_tensor(out=tmp.t[:], in0=p.t[:], scalar=r.t, in1=gB[gname].t[:], op0=ALU.mult, op1=ALU.mult),
                 reads=[p.b, r.b, gB[gname].b], writes=[tmp.b])
            if gname == "g_post_xa" and _os.environ.get("KDBG_SKIP_XA"):
                return
            S.op("dve", lambda e: e.tensor_tensor(out=xs.t[:, xi, :], in0=xs.t[:, xi, :], in1=tmp.t[:], op=ALU.add),
                 reads=[tmp.b, xbuf], writes=[xbuf])

        def tok_proj(lhsT_fn, rd_l, sl, c0, ncols, p):
            fns = []
            off = 0
            while off < ncols:
                n = min(512, ncols - off)
                for k in range(8):
                    fns.append(lambda e, k=k, off=off, n=n: e.matmul(out=p.t[:, off:off + n], lhsT=lhsT_fn(k),
                                                                     rhs=sl.t[:, k, c0 + off:c0 + off + n], start=(k == 0), stop=(k == 7)))
                off += n
            S.pe(fns, reads=rd_l + [sl.b], writes=[p.b])

        def feat_proj(sl, c0, rhsT, col0, n, p, poff=0):
            S.pe([(lambda e, k=k: e.matmul(out=p.t[:, poff:poff + n], lhsT=sl.t[:, k, c0:c0 + 128], rhs=rhsT.t[:, k, col0:col0 + n],
                                           start=(k == 0), stop=(k == 7))) for k in range(8)],
                 reads=[sl.b, rhsT.b], writes=[p.b])

        with ExitStack() as ph:
            mem = sb([128, 2, D], F32, stack=ph)
            memT = sb([128, 8, 256], BF16, stack=ph)
            xnb = sb([128, D], BF16, stack=ph)
            kf = sb([128, D], F32, stack=ph)
            S.dma("sp", mem.t[:], memp.ap().rearrange("(c p) n -> p c n", p=128), writes=[mem.b])
            for mc in range(2):
                norm_T(mem.t[:, mc, :], [mem.b], gcol["g_mem"], memT, mc * 128, xnb)
            for (wsrc, dst_out, is_k) in ((w_xk, omk, True), (w_xv, omv, False)):
                sl = wnext("full")
                for mc in range(2):
                    p = ps2()
                    tok_proj(lambda k, mc=mc: memT.t[:, k, mc * 128:(mc + 1) * 128], [memT.b], sl, 0, D, p)
                    S.op("act", lambda e, p=p: e.activation(out=kf.t[:], in_=p.t[:], func=AF.Copy), reads=[p.b], writes=[kf.b])
                    if not is_k:
                        S.op("dve", lambda e, p=p, mc=mc: e.tensor_copy(out=Vp.t[:, mc, :], in_=p.t[:]), reads=[p.b], writes=[Vp.b])
                    S.dma("sp", dst_out.ap()[mc * 128:(mc + 1) * 128, :], kf.t[:], reads=[kf.b], is_output=True)
                if is_k:
                    for c in range(8):
                        p = ps()
                        feat_proj(sl, c * 128, memT, 0, 256, p)
                        S.op("dve", lambda e, p=p, c=c: e.tensor_copy(out=KTp.t[:, c, :], in_=p.t[:, 0:256]), reads=[p.b], writes=[KTp.b])
                wdone()
        S.barrier()
        _stop_if(1)

        groups = [[("p", i) for i in range(4 * g, 4 * g + 4)] for g in range(4)]
        groups.append([("s", 0)])

        for gi, tiles in enumerate(groups):
            T = len(tiles)
            NT = 128 * T
            has_s = (tiles[0][0] == "s")
            nblocks = [(0, NT)]
            tcol0 = [(ti * 128 if kd == "p" else LP) for (kd, ti) in tiles]

            def xsrc(i):
                kd, ti = tiles[i]
                return (xp.ap()[ti * 128:(ti + 1) * 128, :] if kd == "p" else xsm.ap())

            def ydst(i):
                kd, ti = tiles[i]
                return (yp.ap()[ti * 128:(ti + 1) * 128, :] if kd == "p" else ysm.ap())

            with ExitStack() as ph:
                hT = sb([128, 8, NT], BF16, stack=ph)
                for n_ in ("g_gh", "g_rh", "g_post_mix"):
                    load_gB(n_, ph)
                xnb = sb([128, D], BF16, stack=ph)
                merged = sb([128, T, D], BF16, nb=T, stack=ph)
                mA = sb([128, T, 256], F32, nb=T, stack=ph)
                galT = sb([16, NT], F32, stack=ph)
                onesr = cs["ones"]
                cosg = sb([128, NT], F32, stack=ph)
                sing = sb([128, NT], F32, stack=ph)
                if gi == 0:
                    for i in range(T):
                        S.dma("sp", xs.t[:, i, :], xsrc(i), writes=[xs.bl[i]])
                S.dma("sp", cosg.t[:, 0:NT], cd["cosT"].ap()[:, tcol0[0]:tcol0[0] + NT], writes=[cosg.b])
                S.dma("sp", sing.t[:, 0:NT], cd["sinT"].ap()[:, tcol0[0]:tcol0[0] + NT], writes=[sing.b])
                for i in range(T):
                    norm_T(xs.t[:, i, :], [xs.bl[i]], gcol["g_pre_mix"], hT, i * 128, xnb)

                spb = sb([128, T, 128], F32, stack=ph)
                expb = sb([128, NT], F32, stack=ph)
                expnb = sb([128, NT], F32, stack=ph)
                expr = sb([128, T, 128], F32, stack=ph)
                edec = sb([128, T, 16], F32, stack=ph)
                BS = []
                for b_ in range(2):
                    Bd = {}
                    Bd["keT"] = sb([128, NT], BF16, stack=ph)
                    Bd["Zq"] = [sb([128, (1 if kd == "p" else 16), 128], BF16, stack=ph) for (kd, ti) in tiles]
                    Bd["Zk"] = [sb([128, (1 if kd == "p" else 16), 128], BF16, stack=ph) for (kd, ti) in tiles]
                    for z in Bd["Zq"]:
                        S.op("dve", lambda e, z=z: e.memset(z.t[:], 0.0), writes=[z.b])
                    Bd["vb"] = sb([128, T, 512], BF16, nb=T, stack=ph)
                    Bd["gg"] = sb([128, T, 256], F32, nb=T, stack=ph)
                    if b_ == 0:
                        Bd["ATs"] = [PV(xnb.t[:, i_ * 128:(i_ + 1) * 128], Buf("AT%d" % i_)) for i_ in range(4)]
                    else:
                        at2 = sb([128, 512], BF16, stack=ph)
                        Bd["ATs"] = [PV(at2.t[:, i_ * 128:(i_ + 1) * 128], Buf("ATb%d" % i_)) for i_ in range(4)]
                    Bd["og"] = sb([128, 256], F32, stack=ph)
                    Bd["bnst"] = sb([128, 8], F32, stack=ph)
                    BS.append(Bd)
                gg = BS[0]["gg"]
                sg1s = [sb([128, 512], F32, stack=ph) for _ in range(2)]
                sgi = [0]
                kd32s = [PV(xnb.t[:, 512 + j_ * 256:512 + (j_ + 1) * 256].bitcast(F32), Buf("kd%d" % j_)) for j_ in range(2)]
                t1 = sb([128, 512], F32, stack=ph)
                t2 = sb([128, 512], F32, stack=ph)
                if has_s:
                    Sin = sb([128, NSB, 256], F32, stack=ph)
                    Sinb = sb([128, NSB, 256], BF16, stack=ph)

                    def f32q(tb):
                        return tb.t[:].bitcast(F32).rearrange("p a b -> p (a b)").rearrange("p (q v) -> p q v", v=256)
                    SQ = [[PV(Sin.t[:, 4 * q_:4 * q_ + 4, :], Buf("sq0_%d" % q_)) for q_ in range(4)],
                          [PV(f32q(tb_), Buf("sq1_%d" % q_)) for q_, tb_ in enumerate((KTp, Vp, Sbf[0], Sbf[1]))]]
                    SBS = [PV(Sinb.t[:], Buf("sinb0")),
                           PV(Sst.t[:].bitcast(BF16).rearrange("p a b -> p (a b)").rearrange("p (j v) -> p j v", v=256), Buf("sinb1"))]

                    def load_states(hh_):
                        src_ = (sgla if hh_ % 2 == 0 else sret).ap()[:, hh_ // 2, :, :].rearrange("b d v -> d b v")
                        st_ = hh_ % 2
                        for q_ in range(4):
                            S.dma("sp", SQ[st_][q_].t, src_[:, 4 * q_:4 * q_ + 4, :], writes=[SQ[st_][q_].b])

                    def cast_states(hh_):
                        st_ = hh_ % 2
                        for q_ in range(4):
                            if q_ % 2 == 0:
                                S.op("act", lambda e, q_=q_, st_=st_: e.activation(out=SBS[st_].t[:, 4 * q_:4 * q_ + 4, :], in_=SQ[st_][q_].t, func=AF.Copy),
                                     reads=[SQ[st_][q_].b], writes=[SBS[st_].b])
                            else:
                                S.op("dve", lambda e, q_=q_, st_=st_: e.tensor_copy(out=SBS[st_].t[:, 4 * q_:4 * q_ + 4, :], in_=SQ[st_][q_].t),
                                     reads=[SQ[st_][q_].b], writes=[SBS[st_].b])
                    load_states(0)
                    cast_states(0)

                def head(hh, Bd):
                    keT = Bd["keT"]; Zq = Bd["Zq"]; Zk = Bd["Zk"]; vb = Bd["vb"]; gg = Bd["gg"]
                    ATs = Bd["ATs"]; og = Bd["og"]; bnst = Bd["bnst"]

                    def zq_diag(i):
                        kd = tiles[i][0]
                        nch, c_ = (1, 128) if kd == "p" else (16, 8)
                        return bass.AP(Zq[i].t, 0, [[nch * 128, 128], [128 + c_, nch], [1, c_]]), nch, c_

                    h = hh // 2
                    is_gla = (hh % 2 == 0)
                    sl = wnext("head")
                    if is_gla:
                        tm0, tmn = 128, 896
                    else:
                        tm0, tmn = 512, 768
                    if hh == 0:
                        for (c0, n) in nblocks:
                            p = ps()
                            S.pe([(lambda e, k=k, c0=c0, n=n, p=p: e.matmul(out=p.t[0:16, 0:n], lhsT=sl.t[:, k, 1024:1040], rhs=hT.t[:, k, c0:c0 + n],
                                                                          start=(k == 0), stop=(k == 7))) for k in range(8)],
                                 reads=[sl.b, hT.b], writes=[p.b])
                            S.op("act", lambda e, p=p, c0=c0, n=n: e.activation(out=galT.t[:, c0:c0 + n], in_=p.t[0:16, 0:n], func=AF.Copy),
                                 reads=[p.b], writes=[galT.b])

                    if is_gla:
                        pqk = {}
                        for (c0, n) in nblocks:
                            pq = ps(); pk = ps()
                            feat_proj(sl, 0, hT, c0, n, pq)
                            feat_proj(sl, 128, hT, c0, n, pk)
                            pqk[c0] = (pq, pk)
                        p = ps()
                        fns = []
                        for i in range(T):
                            fns.append(lambda e, i=i: e.matmul(out=p.t[:, i * 128:(i + 1) * 128], lhsT=galT.t[:, i * 128:(i + 1) * 128],
                                                               rhs=wa2.t[:, h * 128:(h + 1) * 128], start=True, stop=False))
                            fns.append(lambda e, i=i: e.matmul(out=p.t[:, i * 128:(i + 1) * 128], lhsT=onesr.t[0:1, :],
                                                               rhs=bga.t[0:1, h * 128:(h + 1) * 128], start=False, stop=True))
                        S.pe(fns, reads=[galT.b, wa2.b, bga.b, onesr.b], writes=[p.b])
                        S.op("act", lambda e, p=p: e.activation(out=spb.t[:].rearrange("p t d -> p (t d)"), in_=p.t[:, 0:NT], func=AF.Exp, scale=-1.0),
                             reads=[p.b], writes=[spb.b])
                        S.op("act", lambda e: e.activation(out=spb.t[:].rearrange("p t d -> p (t d)"), in_=spb.t[:].rearrange("p t d -> p (t d)"),
                                                            func=AF.Ln, bias=1.0), reads=[spb.b], writes=[spb.b])
                        pb = ps(); pl = ps(); pr = ps()
                        fb = []; fl = []; fr = []
                        for i in range(T):
                            kd = tiles[i][0]
                            nch = 1 if kd == "p" else 16
                            mc_ = cs["mcum_" + kd]; mr_ = cs["mrem_" + kd]
                            fb.append(lambda e, i=i, mc_=mc_: e.matmul(out=pb.t[:, i * 128:(i + 1) * 128], lhsT=spb.t[:, i, :], rhs=mc_.t[:, 0:128], start=True, stop=True))
                            fl.append(lambda e, i=i, mc_=mc_, nch=nch: e.matmul(out=pl.t[:, i * 16:i * 16 + nch], lhsT=spb.t[:, i, :], rhs=mc_.t[:, 128:128 + nch], start=True, stop=True))
                            fr.append(lambda e, i=i, mr_=mr_: e.matmul(out=pr.t[:, i * 128:(i + 1) * 128], lhsT=mr_.t[:], rhs=spb.t[:, i, :], start=True, stop=True))
                        rdm = [spb.b, cs["mcum_p"].b, cs["mcum_s"].b, cs["mrem_p"].b, cs["mrem_s"].b]
                        S.pe(fb, reads=rdm, writes=[pb.b])
                        S.pe(fl, reads=rdm, writes=[pl.b])
                        S.pe(fr, reads=rdm, writes=[pr.b])
                        S.op("act", lambda e, pb=pb: e.activation(out=expb.t[:], in_=pb.t[:, 0:NT], func=AF.Exp), reads=[pb.b], writes=[expb.b])
                        S.op("act", lambda e, pb=pb: e.activation(out=expnb.t[:], in_=pb.t[:, 0:NT], func=AF.Exp, scale=-1.0), reads=[pb.b], writes=[expnb.b])
                        S.op("act", lambda e, pl=pl: e.activation(out=edec.t[:].rearrange("p t j -> p (t j)"), in_=pl.t[:, 0:T * 16], func=AF.Exp),
                             reads=[pl.b], writes=[edec.b])
                        S.op("act", lambda e, pr=pr: e.activation(out=expr.t[:].rearrange("p t d -> p (t d)"), in_=pr.t[:, 0:NT], func=AF.Exp),
                             reads=[pr.b], writes=[expr.b])
                        for (c0, n) in nblocks:
                            pq, pk = pqk[c0]
                            for i in range(c0 // 128, (c0 + n) // 128):
                                zap, nch, c_ = zq_diag(i)
                                S.op("dve", lambda e, i=i, zap=zap, nch=nch, pq=pq, c0=c0: e.scalar_tensor_tensor(
                                    out=zap, in0=pq.t[:, i * 128 - c0:(i + 1) * 128 - c0].rearrange("p (j c) -> p j c", j=nch),
                                    scalar=128.0 ** -0.5, in1=expb.t[:, i * 128:(i + 1) * 128].rearrange("p (j c) -> p j c", j=nch),
                                    op0=ALU.mult, op1=ALU.mult), reads=[pq.b, expb.b], writes=[Zq[i].b])
                            S.op("dve", lambda e, pk=pk, c0=c0, n=n: e.tensor_tensor(out=keT.t[:, c0:c0 + n], in0=pk.t[:, 0:n], in1=expnb.t[:, c0:c0 + n], op=ALU.mult),
                                 reads=[pk.b, expnb.b], writes=[keT.b])
                    else:
                        for (c0, n) in nblocks:
                            pq = ps(); pq2 = ps(); pk = ps(); pk2 = ps()
                            feat_proj(sl, 0, hT, c0, n, pq)
                            feat_proj(sl, 128, hT, c0, n, pq2)
                            S.op("dve", lambda e, pq=pq, c0=c0, n=n: e.tensor_tensor(out=t1.t[:, 0:n], in0=pq.t[:, 0:n], in1=cosg.t[:, c0:c0 + n], op=ALU.mult),
                                 reads=[pq.b, cosg.b], writes=[t1.b])
                            S.op("dve", lambda e, pq2=pq2, c0=c0, n=n: e.tensor_tensor(out=t2.t[:, 0:n], in0=pq2.t[:, 0:n], in1=sing.t[:, c0:c0 + n], op=ALU.mult),
                                 reads=[pq2.b, sing.b], writes=[t2.b])
                            S.op("dve", lambda e, n=n: e.tensor_tensor(out=t1.t[:, 0:n], in0=t1.t[:, 0:n], in1=t2.t[:, 0:n], op=ALU.add),
                                 reads=[t1.b, t2.b], writes=[t1.b])
                            for i in range(c0 // 128, (c0 + n) // 128):
                                zap, nch, c_ = zq_diag(i)
                                kd = tiles[i][0]
                                S.op("dve", lambda e, i=i, zap=zap, nch=nch, c0=c0, kd=kd: e.tensor_tensor(
                                    out=zap, in0=t1.t[:, i * 128 - c0:(i + 1) * 128 - c0].rearrange("p (j c) -> p j c", j=nch),
                                    in1=cs["gq_" + kd].t[:, h, :].rearrange("p (j c) -> p j c", j=nch), op=ALU.mult),
                                    reads=[t1.b, cs["gq_" + kd].b], writes=[Zq[i].b])
                            feat_proj(sl, 256, hT, c0, n, pk)
                            feat_proj(sl, 384, hT, c0, n, pk2)
                            S.op("dve", lambda e, pk=pk, c0=c0, n=n: e.tensor_tensor(out=t1.t[:, 0:n], in0=pk.t[:, 0:n], in1=cosg.t[:, c0:c0 + n], op=ALU.mult),
                                 reads=[pk.b, cosg.b], writes=[t1.b])
                            S.op("dve", lambda e, pk2=pk2, c0=c0, n=n: e.tensor_tensor(out=t2.t[:, 0:n], in0=pk2.t[:, 0:n], in1=sing.t[:, c0:c0 + n], op=ALU.mult),
                                 reads=[pk2.b, sing.b], writes=[t2.b])
                            S.op("dve", lambda e, n=n: e.tensor_tensor(out=t1.t[:, 0:n], in0=t1.t[:, 0:n], in1=t2.t[:, 0:n], op=ALU.add),
                                 reads=[t1.b, t2.b], writes=[t1.b])
                            for i in range(c0 // 128, (c0 + n) // 128):
                                kd = tiles[i][0]
                                S.op("dve", lambda e, i=i, c0=c0, kd=kd: e.tensor_tensor(
                                    out=keT.t[:, i * 128:(i + 1) * 128], in0=t1.t[:, i * 128 - c0:(i + 1) * 128 - c0],
                                    in1=cs["gk_" + kd].t[:, h, :], op=ALU.mult), reads=[t1.b, cs["gk_" + kd].b], writes=[keT.b])

                    def stageA(i):
                        kd, ti = tiles[i]
                        nch, c_ = (1, 128) if kd == "p" else (16, 8)
                        pt = ps2()
                        tok_proj(lambda k, i=i: hT.t[:, k, i * 128:(i + 1) * 128], [hT.b], sl, tm0, tmn, pt)
                        vo = 128 if is_gla else 0
                        S.op("act", lambda e, pt=pt, i=i, vo=vo: e.activation(out=vb.t[:, i, :], in_=pt.t[:, vo:vo + 512], func=AF.Copy),
                             reads=[pt.b], writes=[vb.bl[i]])
                        if is_gla and kd == "p":
                            S.op("dve", lambda e, pt=pt, i=i: e.tensor_tensor(out=Zk[i].t[:, 0, :], in0=pt.t[:, 0:128], in1=expr.t[:, i, :], op=ALU.mult),
                                 reads=[pt.b, expr.b], writes=[Zk[i].b])
                        elif is_gla:
                            S.op("dve", lambda e, pt=pt, i=i: e.tensor_tensor(out=kd32s[i % 2].t, in0=pt.t[:, 0:128], in1=expr.t[:, i, :], op=ALU.mult),
                                 reads=[pt.b, expr.b], writes=[kd32s[i % 2].b])
                            S.op("dve", lambda e, i=i, nch=nch, kd=kd: e.tensor_tensor(
                                out=Zk[i].t[:], in0=kd32s[i % 2].t.unsqueeze(1).to_broadcast([128, nch, 128]),
                                in1=cs["ind_" + kd].t[:].unsqueeze(2).to_broadcast([128, nch, 128]), op=ALU.mult),
                                reads=[kd32s[i % 2].b, cs["ind_" + kd].b], writes=[Zk[i].b])
                        else:
                            pT_ = ps()
                            ptv = pT_.t[:].bitcast(BF16)
                            S.pe([lambda e, i=i, ptv=ptv: e.transpose(out=ptv[:, 0:128], in_=keT.t[:, i * 128:(i + 1) * 128], identity=identb.t[:])],
                                 reads=[keT.b, identb.b], writes=[pT_.b])
                            if kd == "p":
                                gam_c = float(np.exp(np.log1p(-np.exp2(-5.0 - h)) * 128.0))
                                S.op("act", lambda e, i=i, ptv=ptv, gam_c=gam_c: e.activation(out=Zk[i].t[:, 0, :], in_=ptv[:, 0:128], func=AF.Copy, scale=gam_c),
                                     reads=[pT_.b], writes=[Zk[i].b])
                            else:
                                S.op("dve", lambda e, i=i, nch=nch, kd=kd, ptv=ptv: e.tensor_tensor(
                                    out=Zk[i].t[:], in0=ptv[:, 0:128].unsqueeze(1).to_broadcast([128, nch, 128]),
                                    in1=cs["wk_" + kd].t[:, h, :].unsqueeze(2).to_broadcast([128, nch, 128]), op=ALU.mult),
                                    reads=[pT_.b, cs["wk_" + kd].b], writes=[Zk[i].b])
                        zap, _, _ = zq_diag(i)
                        pscr = ps()
                        S.pe([lambda e, i=i, zap=zap, pscr=pscr, nch=nch: e.matmul(out=pscr.t[:, 0:128].rearrange("p (j c) -> p j c", j=nch), lhsT=keT.t[:, i * 128:(i + 1) * 128], rhs=zap, start=True, stop=True)],
                             reads=[keT.b, Zq[i].b], writes=[pscr.b])
                        S.op("dve", lambda e, pscr=pscr, kd=kd: e.tensor_tensor(out=ATs[i].t, in0=pscr.t[:, 0:128], in1=cs["caus_" + kd].t[:], op=ALU.mult),
                             reads=[pscr.b, cs["caus_" + kd].b], writes=[ATs[i].b])
                        sg1 = sg1s[sgi[0] % 2]
                        sgi[0] += 1
                        S.op("act", lambda e, pt=pt, vo=vo, sg1=sg1: e.activation(out=sg1.t[:, 0:512], in_=pt.t[:, vo + 256:vo + 768], func=AF.Exp, scale=-1.0),
                             reads=[pt.b], writes=[sg1.b])
                        S.op("act", lambda e, sg1=sg1: e.activation(out=sg1.t[:, 0:512], in_=sg1.t[:, 0:512], func=AF.Ln, bias=1.0),
                             reads=[sg1.b], writes=[sg1.b])
                        S.op("dve", lambda e, sg1=sg1: e.tensor_tensor(out=sg1.t[:, 0:256], in0=sg1.t[:, 0:256], in1=sg1.t[:, 256:512], op=ALU.add),
                             reads=[sg1.b], writes=[sg1.b])
                        S.op("act", lambda e, sg1=sg1: e.activation(out=sg1.t[:, 0:256], in_=sg1.t[:, 0:256], func=AF.Exp, scale=-1.0),
                             reads=[sg1.b], writes=[sg1.b])
                        S.op("dve", lambda e, i=i, sg1=sg1: e.tensor_tensor(out=gg.t[:, i, :], in0=vb.t[:, i, 256:512], in1=sg1.t[:, 0:256], op=ALU.mult),
                             reads=[vb.bl[i], sg1.b], writes=[gg.bl[i]])

                    def stageB1(i):
                        kd, ti = tiles[i]
                        if kd != "p":
                            return

                        def dec_ap(j, i=i):
                            if is_gla:
                                return edec.t[:, i, j:j + 1]
                            return cs["ed_" + kd].t[:, h:h + 1]
                        s_nxt = Sbf[(i + 1) % 2]
                        pu = ps()
                        S.pe([lambda e, i=i, pu=pu: e.matmul(out=pu.t[:, 0:256], lhsT=Zk[i].t[:, 0, :], rhs=vb.t[:, i, 0:256], start=True, stop=True)],
                             reads=[Zk[i].b, vb.bl[i]], writes=[pu.b])
                        d0 = dec_ap(0)
                        S.op("dve", lambda e, pu=pu, d0=d0: e.scalar_tensor_tensor(out=Sst.t[:, hh, :], in0=Sst.t[:, hh, :], scalar=d0, in1=pu.t[:, 0:256],
                                                                                 op0=ALU.mult, op1=ALU.add),
                             reads=[pu.b, edec.b, cs['ed_p'].b, Sst.bl[hh]], writes=[Sst.bl[hh]])
                        S.op("act", lambda e, s_nxt=s_nxt: e.activation(out=s_nxt.t[:, hh, :], in_=Sst.t[:, hh, :], func=AF.Copy),
                             reads=[Sst.bl[hh]], writes=[s_nxt.bl[hh]])

                    def stageB2(i):
                        kd, ti = tiles[i]
                        nch, c_ = (1, 128) if kd == "p" else (16, 8)
                        def dec_ap(j, i=i):
                            if is_gla:
                                return edec.t[:, i, j:j + 1]
                            return cs["ed_" + kd].t[:, h:h + 1]

                        po = ps()
                        if kd == "p":
                            s_cur = Sbf[i % 2]
                            S.pe([lambda e, i=i, po=po: e.matmul(out=po.t[:, 0:256], lhsT=ATs[i].t, rhs=vb.t[:, i, 0:256], start=True, stop=False),
                                  lambda e, i=i, po=po, s_cur=s_cur: e.matmul(out=po.t[:, 0:256], lhsT=Zq[i].t[:, 0, :], rhs=s_cur.t[:, hh, :], start=False, stop=True)],
                                 reads=[ATs[i].b, vb.bl[i], Zq[i].b, s_cur.bl[hh]], writes=[po.b])
                        else:
                            if hh + 1 < 8:
                                load_states(hh + 1)
                            sbs = SBS[hh % 2]
                            fns = [lambda e, i=i, po=po: e.matmul(out=po.t[:, 0:256], lhsT=ATs[i].t, rhs=vb.t[:, i, 0:256], start=True, stop=False)]
                            for j in range(NSB):
                                fns.append(lambda e, i=i, j=j, po=po: e.matmul(out=po.t[:, 0:256], lhsT=Zq[i].t[:, j, :], rhs=sbs.t[:, j, :],
                                                                               start=False, stop=(j == NSB - 1)))
                            S.pe(fns, reads=[ATs[i].b, vb.bl[i], Zq[i].b, sbs.b], writes=[po.b])

                        def sample_updates(i=i, kd=kd, dec_ap=dec_ap):
                            ssrc = sgla if is_gla else sret
                            sdst = oglas if is_gla else orets
                            for j in range(NSB):
                                pu = ps()
                                S.pe([lambda e, i=i, j=j, pu=pu: e.matmul(out=pu.t[:, 0:256], lhsT=Zk[i].t[:, j, :], rhs=vb.t[:, i, 0:256], start=True, stop=True)],
                                     reads=[Zk[i].b, vb.bl[i]], writes=[pu.b])
                                dj = dec_ap(j)
                                sq = SQ[hh % 2][j // 4]
                                S.op("dve", lambda e, pu=pu, dj=dj, j=j, sq=sq: e.scalar_tensor_tensor(out=sq.t[:, j % 4, :], in0=sq.t[:, j % 4, :], scalar=dj, in1=pu.t[:, 0:256],
                                                                                                     op0=ALU.mult, op1=ALU.add),
                                     reads=[pu.b, edec.b, cs['ed_s'].b, sq.b], writes=[sq.b])
                                if j % 4 == 3:
                                    q_ = j // 4
                                    S.dma("sp", sdst.ap()[4 * q_:4 * q_ + 4, h, :, :].rearrange("b d v -> d b v"), sq.t, reads=[sq.b], is_output=True)
                            if hh + 1 < 8:
                                cast_states(hh + 1)

                        if is_gla:
                            ss = sscol()
                            S.op("act", lambda e, po=po, ss=ss: e.activation(out=junk.t[:, 0:256], in_=po.t[:, 0:256], func=AF.Square, accum_out=ss.t),
                                 reads=[po.b], writes=[ss.b])
                            r = rstd_from(ss, 256)
                            S.op("dve", lambda e, po=po, r=r: e.scalar_tensor_tensor(out=og.t[:], in0=po.t[:, 0:256], scalar=r.t, in1=gB["g_gh"].t[:, h * 256:(h + 1) * 256],
                                                                                   op0=ALU.mult, op1=ALU.mult),
                                 reads=[po.b, r.b, gB["g_gh"].b], writes=[og.b])
                            S.op("dve", lambda e, i=i: e.tensor_tensor(out=mA.t[:, i, :], in0=og.t[:], in1=gg.t[:, i, :], op=ALU.mult),
                                 reads=[og.b, gg.bl[i]], writes=[mA.bl[i]])
                        else:
                            S.op("dve", lambda e, po=po: e.bn_stats(out=bnst.t[:, 0:6], in_=po.t[:, 0:256]), reads=[po.b], writes=[bnst.b])
                            S.op("dve", lambda e: e.bn_aggr(out=bnst.t[:, 6:8], in_=bnst.t[:, 0:6]), reads=[bnst.b], writes=[bnst.b])
                            t1c = sscol(); t2c = sscol()
                            S.op("act", lambda e, t2c=t2c: e.activation(out=t2c.t, in_=bnst.t[:, 7:8], func=AF.Ln, bias=EPS), reads=[bnst.b], writes=[t2c.b])
                            S.op("act", lambda e, t2c=t2c: e.activation(out=t2c.t, in_=t2c.t, func=AF.Exp, scale=-0.5), reads=[t2c.b], writes=[t2c.b])
                            S.op("dve", lambda e, t1c=t1c, t2c=t2c: e.scalar_tensor_tensor(out=t1c.t, in0=bnst.t[:, 6:7], scalar=-1.0, in1=t2c.t, op0=ALU.mult, op1=ALU.mult),
                                 reads=[bnst.b, t2c.b], writes=[t1c.b])
                            S.op("act", lambda e, po=po, t1c=t1c, t2c=t2c: e.activation(out=og.t[:], in_=po.t[:, 0:256], func=AF.Identity, scale=t2c.t, bias=t1c.t),
                                 reads=[po.b, t1c.b, t2c.b], writes=[og.b])
                            S.op("dve", lambda e: e.tensor_tensor(out=og.t[:], in0=og.t[:], in1=gB["g_rh"].t[:, h * 256:(h + 1) * 256], op=ALU.mult),
                                 reads=[og.b, gB["g_rh"].b], writes=[og.b])
                            S.op("dve", lambda e, i=i: e.tensor_tensor(out=og.t[:], in0=og.t[:], in1=gg.t[:, i, :], op=ALU.mult),
                                 reads=[og.b, gg.bl[i]], writes=[og.b])
                            S.op("dve", lambda e, i=i: e.tensor_tensor(out=merged.t[:, i, h * 256:(h + 1) * 256], in0=og.t[:], in1=mA.t[:, i, :], op=ALU.add),
                                 reads=[og.b, mA.bl[i]], writes=[merged.bl[i]])
                        if kd == "s":
                            sample_updates()

                    return stageA, stageB1, stageB2

                if has_s:
                    for hh in range(8):
                        sA, sB1, sB2 = head(hh, BS[hh % 2])
                        sA(0)
                        sB1(0)
                        sB2(0)
                        wdone()
                else:
                    for h_ in range(4):
                        S.tag = "g%d.p%d.prolG" % (gi, h_)
                        gA, gB1, gB2 = head(2 * h_, BS[0])
                        S.tag = "g%d.p%d.prolR" % (gi, h_)
                        rA, rB1, rB2 = head(2 * h_ + 1, BS[1])
                        S.tag = "g%d.p%d.A01" % (gi, h_)
                        AH = 2
                        for i0_ in range(min(AH, T)):
                            gA(i0_)
                            rA(i0_)
                        if T <= AH:
                            wdone()
                            wdone()
                        for i in range(T):
                            S.tag = "g%d.p%d.B1g.t%d" % (gi, h_, i); gB1(i)
                            S.tag = "g%d.p%d.B1r.t%d" % (gi, h_, i); rB1(i)
                            if i + AH < T:
                                S.tag = "g%d.p%d.Ag.t%d" % (gi, h_, i + AH); gA(i + AH)
                                S.tag = "g%d.p%d.Ar.t%d" % (gi, h_, i + AH); rA(i + AH)
                                if i + AH == T - 1:
                                    wdone()
                                    wdone()
                            S.tag = "g%d.p%d.B2g.t%d" % (gi, h_, i); gB2(i)
                            S.tag = "g%d.p%d.B2r.t%d" % (gi, h_, i); rB2(i)
                        S.tag = "g%d.other" % gi

                if gi == 3:
                    for hh in range(8):
                        dst = oglap if hh % 2 == 0 else oretp
                        S.dma("sp", dst.ap()[hh // 2, :, :], Sst.t[:, hh, :], reads=[Sst.bl[hh]], is_output=True)

                mT = hT
                for i in range(T):
                    p = ps()
                    pv = p.t[:].bitcast(BF16)
                    S.pe([(lambda e, k=k, i=i, pv=pv: e.transpose(out=pv[:, k * 128:(k + 1) * 128], in_=merged.t[:, i, k * 128:(k + 1) * 128], identity=identb.t[:]))
                          for k in range(8)], reads=[merged.bl[i], identb.b], writes=[p.b])
                    S.op("act", lambda e, i=i, pv=pv: e.activation(out=mT.t[:, :, i * 128:(i + 1) * 128], in_=pv[:, 0:1024].rearrange("p (k t) -> p k t", k=8), func=AF.Copy),
                         reads=[p.b], writes=[mT.b])

                sl = wnext("full")
                if T == 4:
                    t1x = TB(gg.t[:, 0:4, :].rearrange("p t d -> p (t d)"), "t1x")
                    t1x.b = gg.b
                    t1y = TB(BS[1]["gg"].t[:, 0:4, :].rearrange("p t d -> p (t d)"), "t1y")
                    t1y.b = BS[1]["gg"].b
                else:
                    t1x = sb([128, D], F32, stack=ph)
                    t1y = t1x
                if _os.environ.get("KDBG_SBUF"):
                    print("SBUF free in mixer phase (gi=%d):" % gi, nc.sbuf_bytes_remaining)
                for i in range(T):
                    p = ps2()
                    tok_proj(lambda k, i=i: mT.t[:, k, i * 128:(i + 1) * 128], [mT.b], sl, 0, D, p)
                    post_norm_add(p, "g_post_mix", i, xs.bl[i], t1x if i % 2 == 0 else t1y)
                wdone()
            S.barrier()
            _stop_if(10 * (gi + 1) + 1)

            with ExitStack() as ph:
                h2T = sb([128, 8, NT], BF16, stack=ph)
                xnb = sb([128, D], BF16, stack=ph)
                qxT = sb([128, 8, NT], BF16, stack=ph)
                oxT = sb([128, 8, NT], BF16, stack=ph)
                pTb = sb([128, 2, 512], BF16, stack=ph)
                rec = sb([128, 512], F32, stack=ph)
                tmpx = sb([128, D], F32, stack=ph)
                tmpx2 = sb([128, D], F32, stack=ph)
                load_gB("g_post_xa", ph)
                for i in range(T):
                    norm_T(xs.t[:, i, :], [xs.bl[i]], gcol["g_pre_xa"], h2T, i * 128, xnb)
                sl = wnext("full")
                for (c0, n) in nblocks:
                    for c in range(8):
                        p = ps()
                        feat_proj(sl, c * 128, h2T, c0, n, p)
                        eng = "act" if c % 2 == 0 else "dve"
                        if eng == "act":
                            S.op("act", lambda e, p=p, c=c, c0=c0, n=n: e.activation(out=qxT.t[:, c, c0:c0 + n], in_=p.t[:, 0:n], func=AF.Copy), reads=[p.b], writes=[qxT.b])
                        else:
                            S.op("dve", lambda e, p=p, c=c, c0=c0, n=n: e.tensor_copy(out=qxT.t[:, c, c0:c0 + n], in_=p.t[:, 0:n]), reads=[p.b], writes=[qxT.b])
                wdone()

                pTbufs = [Buf("pTb0"), Buf("pTb1")]
                rbufs = [Buf("rec0"), Buf("rec1")]

                def attend(segs, c0, n, KT_of, V_of, rdk, heads=(0, 1, 2, 3), pcol=0, blk=0):
                    pb_ = pTbufs[blk]; rb_ = rbufs[blk]
                    for h in heads:
                        psc = ps2()
                        fns = []
                        for si, (so, sw) in enumerate(segs):
                            for mc in range(2):
                                for dc in range(2):
                                    fns.append(lambda e, si=si, so=so, sw=sw, mc=mc, dc=dc: e.matmul(
                                        out=psc.t[:, mc * 512 + so:mc * 512 + so + sw], lhsT=KT_of(si, 2 * h + dc, mc),
                                        rhs=qxT.t[:, 2 * h + dc, c0 + so:c0 + so + sw], start=(dc == 0), stop=(dc == 1)))
                        S.pe(fns, reads=rdk + [qxT.b], writes=[psc.b])
                        S.op("act", lambda e, psc=psc: e.activation(out=pTb.t[:, :, pcol:pcol + n], in_=psc.t[:].rearrange("p (m t) -> p m t", m=2)[:, :, 0:n],
                                                                      func=AF.Exp, scale=1.0 / 16.0), reads=[psc.b], writes=[pb_])
                        pov = ps2(); pden = ps()
                        fns = []
                        for si, (so, sw) in enumerate(segs):
                            for dc in range(2):
                                for mc in range(2):
                                    fns.append(lambda e, si=si, so=so, sw=sw, mc=mc, dc=dc: e.matmul(
                                        out=pov.t[:, dc * 512 + so:dc * 512 + so + sw], lhsT=V_of(si, mc, h * 256 + dc * 128),
                                        rhs=pTb.t[:, mc, pcol + so:pcol + so + sw], start=(mc == 0), stop=(mc == 1)))
                        S.pe(fns, reads=rdk + [pb_], writes=[pov.b])
                        S.pe([(lambda e, mc=mc: e.matmul(out=pden.t[:, 0:n], lhsT=onesb.t[:], rhs=pTb.t[:, mc, pcol:pcol + n], start=(mc == 0), stop=(mc == 1)))
                              for mc in range(2)], reads=[onesb.b, pb_], writes=[pden.b])
                        S.op("dve", lambda e, pden=pden: e.reciprocal(out=rec.t[:, pcol:pcol + n], in_=pden.t[:, 0:n]), reads=[pden.b], writes=[rb_])
                        for dc in range(2):
                            S.op("dve", lambda e, pov=pov, dc=dc, h=h: e.tensor_tensor(out=oxT.t[:, 2 * h + dc, c0:c0 + n], in0=pov.t[:, dc * 512:dc * 512 + n],
                                                                                   in1=rec.t[:, pcol:pcol + n], op=ALU.mult), reads=[pov.b, rb_], writes=[oxT.b])

                if not has_s:
                    attend([(0, 512)], 0, 512,
                           lambda si, c, mc: KTp.t[:, c, mc * 128:(mc + 1) * 128],
                           lambda si, mc, col: Vp.t[:, mc, col:col + 128], [KTp.b, Vp.b])
                else:
                    Kb = sb([128, 4, 2, D], BF16, stack=ph)
                    Vb = sb([128, 4, 2, D], BF16, stack=ph)
                    KTb = sb([128, 4, 8, 256], BF16, stack=ph)
                    for sbi in range(4):
                        S.dma("pool", Kb.t[:], cmk.ap()[sbi * 4:(sbi + 1) * 4].rearrange("b (c p) n -> p b c n", p=128), writes=[Kb.b])
                        S.dma("pool", Vb.t[:], cmv.ap()[sbi * 4:(sbi + 1) * 4].rearrange("b (c p) n -> p b c n", p=128), writes=[Vb.b])
                        for bb in range(4):
                            for half in range(2):
                                p = ps()
                                pv = p.t[:].bitcast(BF16)
                                fns = []
                                for cc in range(4):
                                    c = half * 4 + cc
                                    for mc in range(2):
                                        fns.append(lambda e, bb=bb, c=c, cc=cc, mc=mc, pv=pv: e.transpose(
                                            out=pv[:, cc * 256 + mc * 128:cc * 256 + (mc + 1) * 128], in_=Kb.t[:, bb, mc, c * 128:(c + 1) * 128], identity=identb.t[:]))
                                S.pe(fns, reads=[Kb.b, identb.b], writes=[p.b])
                                S.op("act" if half == 0 else "dve",
                                     (lambda e, bb=bb, half=half, pv=pv: e.activation(out=KTb.t[:, bb, half * 4:(half + 1) * 4, :],
                                                                                   in_=pv[:, 0:1024].rearrange("p (c m) -> p c m", c=4), func=AF.Copy)) if half == 0 else
                                     (lambda e, bb=bb, half=half, pv=pv: e.tensor_copy(out=KTb.t[:, bb, half * 4:(half + 1) * 4, :],
                                                                                    in_=pv[:, 0:1024].rearrange("p (c m) -> p c m", c=4))),
                                     reads=[p.b], writes=[KTb.b])
                        attend([(bb * 8, 8) for bb in range(4)], sbi * 32, 32,
                               lambda si, c, mc: KTb.t[:, si, c, mc * 128:(mc + 1) * 128],
                               lambda si, mc, col: Vb.t[:, si, mc, col:col + 128], [KTb.b, Vb.b])
                sl = wnext("full")
                for i in range(T):
                    p = ps2()
                    tok_proj(lambda k, i=i: oxT.t[:, k, i * 128:(i + 1) * 128], [oxT.b], sl, 0, D, p)
                    post_norm_add(p, "g_post_xa", i, xs.bl[i], tmpx if i % 2 == 0 else tmpx2)
                wdone()
                hfT = sb([128, 8, NT], BF16, stack=ph)
                yacc = sb([128, T, D], F32, nb=T, stack=ph)
                uT = [sb([128, 4, NT], BF16, stack=ph) for _ in range(2)]
                sq = sb([128, 512], BF16, stack=ph)
                load_gB("g_post_ffn", ph)
                for i in range(T):
                    norm_T(xs.t[:, i, :], [xs.bl[i]], gcol["g_pre_ffn"], hfT, i * 128, xnb)
                NJB = 8
                for jb in range(NJB):
                    su = wnext("ffn")
                    u = uT[jb % 2]
                    for (c0, n) in nblocks:
                        for fc in range(4):
                            p = ps()
                            feat_proj(su, fc * 128, hfT, c0, n, p)
                            S.op("act", lambda e, p=p, n=n: e.activation(out=sq.t[:, 0:n], in_=p.t[:, 0:n], func=AF.Square), reads=[p.b], writes=[sq.b])
                            S.op("dve", lambda e, p=p, n=n, fc=fc, c0=c0, u=u: e.scalar_tensor_tensor(out=u.t[:, fc, c0:c0 + n], in0=p.t[:, 0:n], scalar=0.0, in1=sq.t[:, 0:n],
                                                                                                  op0=ALU.is_gt, op1=ALU.mult), reads=[p.b, sq.b], writes=[u.b])
                    for i in range(T):
                        p = ps2()
                        S.pe([(lambda e, fc=fc, hf=hf, i=i, u=u, p=p: e.matmul(out=p.t[:, hf * 512:(hf + 1) * 512], lhsT=u.t[:, fc, i * 128:(i + 1) * 128],
                                                                             rhs=su.t[:, 2 * fc + hf, 512:1024], start=(fc == 0), stop=(fc == 3)))
                              for hf in range(2) for fc in range(4)], reads=[u.b, su.b], writes=[p.b])
                        if jb == 0:
                            S.op("act", lambda e, p=p, i=i: e.activation(out=yacc.t[:, i, :], in_=p.t[:], func=AF.Copy), reads=[p.b], writes=[yacc.bl[i]])
                        elif jb < NJB - 1:
                            S.op("dve", lambda e, p=p, i=i: e.tensor_tensor(out=yacc.t[:, i, :], in0=yacc.t[:, i, :], in1=p.t[:], op=ALU.add),
                                 reads=[p.b, yacc.bl[i]], writes=[yacc.bl[i]])
                        else:
                            S.op("dve", lambda e, p=p, i=i: e.tensor_tensor(out=yacc.t[:, i, :], in0=yacc.t[:, i, :], in1=p.t[:], op=ALU.add),
                                 reads=[p.b, yacc.bl[i]], writes=[yacc.bl[i]])
                            ss = sscol()
                            S.op("act", lambda e, i=i, ss=ss: e.activation(out=junk.t[:], in_=yacc.t[:, i, :], func=AF.Square, accum_out=ss.t),
                                 reads=[yacc.bl[i]], writes=[ss.b])
                            r = rstd_from(ss, D)
                            S.op("dve", lambda e, i=i, r=r: e.scalar_tensor_tensor(out=tmpx.t[:], in0=yacc.t[:, i, :], scalar=r.t, in1=gB["g_post_ffn"].t[:],
                                                                                 op0=ALU.mult, op1=ALU.mult), reads=[yacc.bl[i], r.b, gB["g_post_ffn"].b], writes=[tmpx.b])
                            if not _os.environ.get("KDBG_SKIP_FFN"):
                                S.op("dve", lambda e, i=i: e.tensor_tensor(out=xs.t[:, i, :], in0=xs.t[:, i, :], in1=tmpx.t[:], op=ALU.add),
                                     reads=[tmpx.b, xs.bl[i]], writes=[xs.bl[i]])
                            S.dma("sp", ydst(i), xs.t[:, i, :], reads=[xs.bl[i]], is_output=True)
                    wdone()
                if gi + 1 < len(groups):
                    for i_, (kd_, ti_) in enumerate(groups[gi + 1]):
                        src_ = xp.ap()[ti_ * 128:(ti_ + 1) * 128, :] if kd_ == "p" else xsm.ap()
                        S.dma("sp", xs.t[:, i_, :], src_, writes=[xs.bl[i_]])
            S.barrier()
            _stop_if(10 * (gi + 1) + 3)


_CACHE = {}


def kernel(x_prompt, x_sample, state_gla, state_ret, cache_mem_k, cache_mem_v, mem_prompt,
           w_in, w_gla_a2, b_gla_a, g_gla_head, g_ret_head, w_mix_out,
           w_xq, w_xk, w_xv, w_xo, g_mem, w_up, w_down,
           g_pre_mix, g_post_mix, g_pre_xa, g_post_xa, g_pre_ffn, g_post_ffn):
    f = lambda a: np.ascontiguousarray(np.asarray(a, dtype=np.float32))
    if "nc" not in _CACHE:
        _CACHE["nc"] = build_nc()
    nc, consts = _CACHE["nc"]
    x_prompt = f(x_prompt); x_sample = f(x_sample)
    state_gla = f(state_gla); state_ret = f(state_ret)
    cache_mem_k = f(cache_mem_k); cache_mem_v = f(cache_mem_v); mem_prompt = f(mem_prompt)
    wi = f(w_in)[0]
    w_inh = np.zeros((8, D, SLOTW), np.float32)
    for h in range(4):
        g = w_inh[2 * h]
        g[:, 0:128] = wi[:, O_GQ + h * 128:O_GQ + (h + 1) * 128]
        g[:, 128:256] = wi[:, O_GK + h * 128:O_GK + (h + 1) * 128]
        g[:, 256:512] = wi[:, O_GV + h * 256:O_GV + (h + 1) * 256]
        g[:, 512:768] = wi[:, O_GR + h * 256:O_GR + (h + 1) * 256]
        g[:, 768:1024] = wi[:, O_A + h * 256:O_A + (h + 1) * 256]
        if h == 0:
            g[:, 1024:1040] = wi[:, O_GA:O_GA + 16]
        r = w_inh[2 * h + 1]
        r[:, 0:128] = wi[:, O_RQ + h * 128:O_RQ + (h + 1) * 128]
        r[:, 128:192] = wi[:, O_RQ + h * 128 + 64:O_RQ + (h + 1) * 128]
        r[:, 192:256] = wi[:, O_RQ + h * 128:O_RQ + h * 128 + 64]
        r[:, 256:384] = wi[:, O_RK + h * 128:O_RK + (h + 1) * 128]
        r[:, 384:448] = wi[:, O_RK + h * 128 + 64:O_RK + (h + 1) * 128]
        r[:, 448:512] = wi[:, O_RK + h * 128:O_RK + h * 128 + 64]
        r[:, 512:768] = wi[:, O_RV + h * 256:O_RV + (h + 1) * 256]
        r[:, 768:1024] = wi[:, O_RG + h * 256:O_RG + (h + 1) * 256]
        r[:, 1024:1280] = wi[:, O_B + h * 256:O_B + (h + 1) * 256]
    shared = {
        "w_inh": w_inh, "w_a2": f(w_gla_a2)[0], "b_a": f(b_gla_a)[0].reshape(1, 512),
        "g_gh": f(g_gla_head)[0].reshape(1, D), "g_rh": f(g_ret_head)[0].reshape(1, D),
        "w_mo": f(w_mix_out)[0], "w_xq": f(w_xq)[0], "w_xk": f(w_xk)[0], "w_xv": f(w_xv)[0], "w_xo": f(w_xo)[0],
        "w_up": f(w_up)[0], "w_dn": f(w_down)[0],
        "g_mem": f(g_mem)[0].reshape(1, D), "g_pre_mix": f(g_pre_mix)[0].reshape(1, D),
        "g_post_mix": f(g_post_mix)[0].reshape(1, D), "g_pre_xa": f(g_pre_xa)[0].reshape(1, D),
        "g_post_xa": f(g_post_xa)[0].reshape(1, D), "g_pre_ffn": f(g_pre_ffn)[0].reshape(1, D),
        "g_post_ffn": f(g_post_ffn)[0].reshape(1, D),
    }
    for k, v in consts.items():
        shared["c_" + k] = v
    in_maps = []
    for c in range(NCORE):
        m = dict(shared)
        m["xp"] = x_prompt[c]
        m["xsm"] = x_sample[c * NSB:(c + 1) * NSB].reshape(128, D)
        m["sgla"] = state_gla[0, c * NSB:(c + 1) * NSB]
        m["sret"] = state_ret[0, c * NSB:(c + 1) * NSB]
        m["cmk"] = cache_mem_k[0, c * NSB:(c + 1) * NSB].reshape(NSB, 256, D)
        m["cmv"] = cache_mem_v[0, c * NSB:(c + 1) * NSB].reshape(NSB, 256, D)
        m["memp"] = mem_prompt[c]
        in_maps.append(m)
    res = run_bass_kernel_spmd(nc, in_maps, core_ids=list(range(NCORE)))
    R = res.results
    yp = np.stack([R[c]["yp"] for c in range(NCORE)])
    ys = np.concatenate([R[c]["ysm"].reshape(NSB, 8, D) for c in range(NCORE)], 0)
    glap = np.stack([R[c]["oglap"] for c in range(NCORE)])[None]
    retp = np.stack([R[c]["oretp"] for c in range(NCORE)])[None]
    mk = np.stack([R[c]["omk"].reshape(256, 4, 256) for c in range(NCORE)])[None]
    mv = np.stack([R[c]["omv"].reshape(256, 4, 256) for c in range(NCORE)])[None]
    glas = np.concatenate([R[c]["oglas"] for c in range(NCORE)], 0)[None]
    rets = np.concatenate([R[c]["orets"] for c in range(NCORE)], 0)[None]
    return (yp.astype(np.float32), ys.astype(np.float32), glap.astype(np.float32), retp.astype(np.float32),
            mk.astype(np.float32), mv.astype(np.float32), glas.astype(np.float32), rets.astype(np.float32))
```
